# Optimizing a Trainium2 kernel written in Bass

```python
import jax, jax.numpy as jnp
from jax import lax
import numpy as np

D_MODEL = 1024
BATCH = 1
SEQ = 16384
DEPTH = 2

N_EVEN = (DEPTH + 1) // 2
N_ODD = DEPTH // 2

A_WIDTH = D_MODEL // 2
A_GROUP = 128
A_GROUPS = A_WIDTH // A_GROUP
A_CHUNK = 128
LN_EPS = 1e-5
B_WIDTH = D_MODEL // 2
B_HEAD = 64
B_HEADS = B_WIDTH // B_HEAD
LORA_W = 64
LORA_A = 64
LORA_G = 128
GN_EPS = 64e-5
SHIFT_W = 3 * B_WIDTH + LORA_W + LORA_A + LORA_G
EVEN_IN = 2 * A_WIDTH + SHIFT_W
C_HEAD = 64
C_HEADS = D_MODEL // C_HEAD
C_PATTERNS = ((128, 1), (512, 4), (2048, 16))
Q_BLOCK = 128
NEG_INF = -1e30
D_FF = 2816
CONV_WIDTH = 3
RMS_EPS = 1e-6

kernel_name = "hybrid_gmlp_rwkv7_dilated_attn_convffn"


def rmsnorm(x, g):
    xf = x.astype(jnp.float32)
    y = xf * lax.rsqrt(jnp.mean(xf * xf, axis=-1, keepdims=True) + RMS_EPS)
    return (y * g.astype(jnp.float32)).astype(x.dtype)


def shift_prev(h):
    return jnp.pad(h, ((0, 0), (1, 0), (0, 0)))[:, :-1]


def chunked_spatial_gating(u, v, ln_g, ln_b, w_s, b_s):
    bsz, t_len, _ = v.shape
    vf = v.astype(jnp.float32)
    mean = jnp.mean(vf, axis=-1, keepdims=True)
    var = jnp.mean(jnp.square(vf - mean), axis=-1, keepdims=True)
    vn = (vf - mean) * lax.rsqrt(var + LN_EPS) * ln_g.astype(jnp.float32) + ln_b.astype(jnp.float32)
    vc = vn.reshape(bsz, t_len // A_CHUNK, A_CHUNK, A_GROUPS, A_GROUP)
    causal = jnp.tril(jnp.ones((A_CHUNK, A_CHUNK), jnp.float32))
    mixed = jnp.einsum('gts,bnsgc->bntgc', w_s.astype(jnp.float32) * causal, vc)
    mixed = mixed + b_s.astype(jnp.float32).T[None, None, :, :, None]
    return u * mixed.reshape(bsz, t_len, A_WIDTH).astype(u.dtype)


def rwkv7_time_mix(p, mu, w0, w2, a0, a2, g2, k_k, k_a, r_k, gn_g, gn_b):
    bsz, t_len, _ = p.shape
    p = p + (shift_prev(p) - p) * mu
    cuts = np.cumsum([B_WIDTH, B_WIDTH, B_WIDTH, LORA_W, LORA_A])
    r, k, v, lw, la, lg = jnp.split(p, [int(c) for c in cuts], axis=-1)
    w = -jax.nn.softplus(-(w0 + jnp.tanh(lw) @ w2)) - 0.5
    decay = jnp.exp(-jnp.exp(w.astype(jnp.float32)))
    a = jax.nn.sigmoid(a0 + la @ a2)
    g = jax.nn.sigmoid(lg) @ g2

    def heads(z):
        return z.reshape(bsz, t_len, B_HEADS, B_HEAD).astype(jnp.float32)

    kk = heads(k * k_k)
    kk = kk * lax.rsqrt(jnp.maximum(jnp.sum(kk * kk, -1, keepdims=True), 1e-24))
    k = k * (1.0 + (a - 1.0) * k_a)
    rh, kh, vh, ah, wh = heads(r), heads(k), heads(v), heads(a), decay.reshape(bsz, t_len, B_HEADS, B_HEAD)

    def step(S, inp):
        r_t, w_t, k_t, v_t, kk_t, a_t = inp
        sa = jnp.einsum('bhvk,bhk->bhv', S, -kk_t)
        S = (S * w_t[:, :, None, :] + sa[..., None] * (kk_t * a_t)[:, :, None, :]
             + v_t[..., None] * k_t[:, :, None, :])
        return S, jnp.einsum('bhvk,bhk->bhv', S, r_t)

    tm = lambda z: jnp.swapaxes(z, 0, 1)
    S0 = jnp.zeros((bsz, B_HEADS, B_HEAD, B_HEAD), jnp.float32)
    _, o = lax.scan(step, S0, (tm(rh), tm(wh), tm(kh), tm(vh), tm(kk), tm(ah)))
    o = tm(o)
    mean = jnp.mean(o, -1, keepdims=True)
    var = jnp.mean(jnp.square(o - mean), -1, keepdims=True)
    o = ((o - mean) * lax.rsqrt(var + GN_EPS) * gn_g.reshape(B_HEADS, B_HEAD).astype(jnp.float32)
         + gn_b.reshape(B_HEADS, B_HEAD).astype(jnp.float32))
    bonus = jnp.sum(rh * kh * r_k.astype(jnp.float32), -1, keepdims=True) * vh
    out = (o + bonus).reshape(bsz, t_len, B_WIDTH).astype(p.dtype)
    return out * g


def dilated_attention(q, k, v):
    bsz, t_len = q.shape[:2]
    qf = q.astype(jnp.float32) * (C_HEAD ** -0.5)
    kf = k.astype(jnp.float32)
    vf = v.astype(jnp.float32)

    def block(t0):
        qb = lax.dynamic_slice_in_dim(qf, t0, Q_BLOCK, axis=1)
        i = t0 + jnp.arange(Q_BLOCK)
        outs, lses = [], []
        for window, dil in C_PATTERNS:
            j = jnp.arange(window // dil + 1)
            pos = i[:, None] - dil * j[None, :]
            valid = pos >= 0
            idx = jnp.maximum(pos, 0)
            kg = jnp.take(kf, idx, axis=1)
            vg = jnp.take(vf, idx, axis=1)
            s = jnp.einsum('bihd,bijhd->bhij', qb, kg)
            s = jnp.where(valid[None, None], s, NEG_INF)
            lse = jax.nn.logsumexp(s, axis=-1)
            pr = jnp.exp(s - lse[..., None])
            outs.append(jnp.einsum('bhij,bijhd->bihd', pr, vg))
            lses.append(lse)
        wts = jax.nn.softmax(jnp.stack(lses), axis=0)
        wts = jnp.transpose(wts, (0, 1, 3, 2))[..., None]
        return jnp.sum(wts * jnp.stack(outs), axis=0)

    starts = jnp.arange(t_len // Q_BLOCK) * Q_BLOCK
    ob = lax.map(block, starts)
    out = jnp.swapaxes(ob, 0, 1).reshape(bsz, t_len, C_HEADS * C_HEAD)
    return out.astype(q.dtype)


def conv_glu_ffn(h, w_up, conv_w, conv_b, w_down):
    z = h @ w_up
    z1 = shift_prev(z)
    z2 = shift_prev(z1)
    z = conv_w[0] * z2 + conv_w[1] * z1 + conv_w[2] * z + conv_b
    gate, val = jnp.split(z, 2, axis=-1)
    return (jax.nn.silu(gate) * val) @ w_down


def setup_inputs(seed: int = 0) -> dict:
    key = jax.random.key(seed)
    cnt = [0]

    def nk():
        cnt[0] += 1
        return jax.random.fold_in(key, cnt[0])

    def nrm(shape, scale):
        return scale * jax.random.normal(nk(), shape, jnp.float32)

    def gain(shape):
        return 1.0 + 0.02 * jax.random.normal(nk(), shape, jnp.float32)

    def unif(shape, lo, hi):
        return jax.random.uniform(nk(), shape, jnp.float32, lo, hi)

    D, E, O = D_MODEL, N_EVEN, N_ODD
    return {
        "x": nrm((BATCH, SEQ, D), 1.0),
        "ev_norm": gain((E, D)),
        "ev_w_in": nrm((E, D, EVEN_IN), D ** -0.5),
        "ev_ln_g": gain((E, A_WIDTH)),
        "ev_ln_b": nrm((E, A_WIDTH), 0.02),
        "ev_w_s": nrm((E, A_GROUPS, A_CHUNK, A_CHUNK), A_CHUNK ** -0.5),
        "ev_b_s": gain((E, A_GROUPS, A_CHUNK)),
        "ev_mu": unif((E, SHIFT_W), 0.0, 1.0),
        "ev_w0": unif((E, B_WIDTH), -6.5, -1.5),
        "ev_w2": nrm((E, LORA_W, B_WIDTH), 0.5 * LORA_W ** -0.5),
        "ev_a0": nrm((E, B_WIDTH), 0.1),
        "ev_a2": nrm((E, LORA_A, B_WIDTH), 0.5 * LORA_A ** -0.5),
        "ev_g2": nrm((E, LORA_G, B_WIDTH), LORA_G ** -0.5),
        "ev_k_k": 0.85 + nrm((E, B_WIDTH), 0.02),
        "ev_k_a": gain((E, B_WIDTH)),
        "ev_r_k": nrm((E, B_HEADS, B_HEAD), 0.1),
        "ev_gn_g": gain((E, B_WIDTH)),
        "ev_gn_b": nrm((E, B_WIDTH), 0.02),
        "ev_w_out": nrm((E, D, D), D ** -0.5),
        "od_norm": gain((O, D)),
        "od_w_qkv": nrm((O, D, 3 * D), D ** -0.5),
        "od_w_out": nrm((O, D, D), D ** -0.5),
        "ff_norm": gain((DEPTH, D)),
        "ff_w_up": nrm((DEPTH, D, 2 * D_FF), D ** -0.5),
        "ff_conv_w": nrm((DEPTH, CONV_WIDTH, 2 * D_FF), CONV_WIDTH ** -0.5),
        "ff_conv_b": nrm((DEPTH, 2 * D_FF), 0.02),
        "ff_w_down": nrm((DEPTH, D_FF, D), D_FF ** -0.5),
        "final_norm": gain((D,)),
    }


def reference(x, ev_norm, ev_w_in, ev_ln_g, ev_ln_b, ev_w_s, ev_b_s, ev_mu, ev_w0, ev_w2,
              ev_a0, ev_a2, ev_g2, ev_k_k, ev_k_a, ev_r_k, ev_gn_g, ev_gn_b, ev_w_out,
              od_norm, od_w_qkv, od_w_out, ff_norm, ff_w_up, ff_conv_w, ff_conv_b, ff_w_down,
              final_norm):
    bsz, t_len, _ = x.shape
    for layer in range(DEPTH):
        j = layer // 2
        if layer % 2 == 0:
            h = rmsnorm(x, ev_norm[j])
            p = h @ ev_w_in[j]
            y_a = chunked_spatial_gating(p[..., :A_WIDTH], p[..., A_WIDTH:2 * A_WIDTH],
                                         ev_ln_g[j], ev_ln_b[j], ev_w_s[j], ev_b_s[j])
            y_b = rwkv7_time_mix(p[..., 2 * A_WIDTH:], ev_mu[j], ev_w0[j], ev_w2[j], ev_a0[j],
                                 ev_a2[j], ev_g2[j], ev_k_k[j], ev_k_a[j], ev_r_k[j],
                                 ev_gn_g[j], ev_gn_b[j])
            x = x + jnp.concatenate([y_a, y_b], axis=-1) @ ev_w_out[j]
        else:
            h = rmsnorm(x, od_norm[j])
            qkv = (h @ od_w_qkv[j]).reshape(bsz, t_len, 3, C_HEADS, C_HEAD)
            y = dilated_attention(qkv[:, :, 0], qkv[:, :, 1], qkv[:, :, 2])
            x = x + y @ od_w_out[j]
        x = x + conv_glu_ffn(rmsnorm(x, ff_norm[layer]), ff_w_up[layer], ff_conv_w[layer],
                             ff_conv_b[layer], ff_w_down[layer])
    return rmsnorm(x, final_norm)
```

```python
import math
import numpy as np
import concourse.bass as bass
import concourse.mybir as mybir
from concourse.bass_utils import run_bass_kernel_spmd

F32 = mybir.dt.float32
BF16 = mybir.dt.bfloat16
AF = mybir.ActivationFunctionType
ALU = mybir.AluOpType

NCORE = 8
SEQ = 16384
TPC = SEQ // NCORE
D = 1024
C0 = math.exp(-0.5)
DEBUG = False


class _Eng:
    def __init__(self, nc, name, eng):
        self.name = name
        self.eng = eng
        self.sem = nc.alloc_semaphore(name="s_" + name)
        self.count = 0
        self.seen = {}


class PB:
    NPOOL = 64

    def __init__(self):
        nc = bass.Bass("TRN2", target_bir_lowering=False)
        self.nc = nc
        self.engs = {}
        for name, eng in (("pe", nc.tensor), ("dve", nc.vector), ("act", nc.scalar),
                          ("pool", nc.gpsimd), ("sp", nc.sync)):
            self.engs[name] = _Eng(nc, name, eng)
        self.dpool = [nc.alloc_semaphore(name=f"s_dma{i}") for i in range(self.NPOOL)]
        self.dcount = 0
        self.dseen = {n: set() for n in self.engs}
        self.lastw = {}
        self.readers = {}
        self.ninst = 0

    def din(self, name, shape, dt=F32):
        return self.nc.dram_tensor(name, list(shape), dt, kind="ExternalInput").ap()

    def dout(self, name, shape, dt=F32):
        return self.nc.dram_tensor(name, list(shape), dt, kind="ExternalOutput").ap()

    def sb(self, name, shape, dt=F32):
        return self.nc.alloc_sbuf_tensor("sb_" + name, list(shape), dt).ap()

    def ps(self, name, shape=(128, 512), dt=F32):
        return self.nc.alloc_psum_tensor("pp_" + name, list(shape), dt).ap()

    def _wait(self, ename, tok):
        E = self.engs[ename]
        if tok[0] == "e":
            _, src, idx = tok
            if E.seen.get(src, 0) >= idx:
                return
            E.seen[src] = idx
            E.eng.wait_ge(self.engs[src].sem, idx)
        else:
            j = tok[1]
            if j in self.dseen[ename]:
                return
            self.dseen[ename].add(j)
            E.eng.wait_ge(self.dpool[j % self.NPOOL], 16 * (j // self.NPOOL + 1))

    @staticmethod
    def _keys(aps):
        ks = []
        for a in aps:
            if a is None or isinstance(a, (int, float)):
                continue
            ks.append(a.tensor.name)
        return ks

    def _deps(self, ename, reads, writes, acc):
        toks = []
        for k in reads:
            t = self.lastw.get(k)
            if t is not None:
                toks.append(t)
        for k in writes:
            t = self.lastw.get(k)
            if t is not None and not (acc and t[0] == "e" and t[1] == ename):
                toks.append(t)
            toks.extend(self.readers.get(k, ()))
        for t in toks:
            self._wait(ename, t)

    def _record(self, tok, reads, writes):
        for k in reads:
            self.readers.setdefault(k, []).append(tok)
        for k in writes:
            self.lastw[k] = tok
            self.readers[k] = []

    def op(self, ename, fn, reads, writes, acc=False):
        reads = self._keys(reads)
        writes = self._keys(writes)
        E = self.engs[ename]
        self._deps(ename, reads, writes, acc)
        ins = fn(E.eng)
        E.count += 1
        self.ninst += 1
        ins.then_inc(E.sem, 1)
        self._record(("e", ename, E.count), reads, writes)

    def dma(self, out, in_, eng="sp"):
        E = self.engs[eng]
        j = self.dcount
        self.dcount += 1
        if j >= self.NPOOL:
            self._wait(eng, ("d", j - self.NPOOL))
        reads = self._keys([in_])
        writes = self._keys([out])
        self._deps(eng, reads, writes, False)
        ins = E.eng.dma_start(out=out, in_=in_)
        ins.then_inc(self.dpool[j % self.NPOOL], 16)
        self.ninst += 1
        self._record(("d", j), reads, writes)

    def finish(self):
        for k, t in list(self.lastw.items()):
            self._wait("sp", t)
        for k, ts in list(self.readers.items()):
            for t in ts:
                self._wait("sp", t)

    def mm(self, out, lhsT, rhs, start=True, stop=True):
        self.op("pe", lambda e: e.matmul(out, lhsT=lhsT, rhs=rhs, start=start, stop=stop),
                [lhsT, rhs], [out], acc=not start)

    def tr(self, out, in_, ident):
        self.op("pe", lambda e: e.transpose(out, in_, ident), [in_, ident], [out])

    def act(self, out, in_, func, bias=None, scale=None):
        kw = {}
        if bias is not None:
            kw["bias"] = bias
        if scale is not None:
            kw["scale"] = scale
        self.op("act", lambda e: e.activation(out=out, in_=in_, func=func, **kw),
                [in_, bias, scale], [out])

    def cp(self, out, in_, eng="act"):
        if eng == "act":
            self.op("act", lambda e: e.copy(out=out, in_=in_), [in_], [out])
        else:
            self.op(eng, lambda e: e.tensor_copy(out=out, in_=in_), [in_], [out])

    def tt(self, out, in0, in1, op, eng="dve"):
        self.op(eng, lambda e: e.tensor_tensor(out=out, in0=in0, in1=in1, op=op), [in0, in1], [out])

    def ts(self, out, in0, s1, s2, op0, op1=None, eng="dve"):
        if op1 is None:
            self.op(eng, lambda e: e.tensor_scalar(out=out, in0=in0, scalar1=s1, scalar2=None, op0=op0),
                    [in0, s1], [out])
        else:
            self.op(eng, lambda e: e.tensor_scalar(out=out, in0=in0, scalar1=s1, scalar2=s2, op0=op0, op1=op1),
                    [in0, s1, s2], [out])

    def stt(self, out, in0, s, in1, op0, op1):
        self.op("dve", lambda e: e.scalar_tensor_tensor(out=out, in0=in0, scalar=s, in1=in1, op0=op0, op1=op1),
                [in0, s, in1], [out])

    def memset(self, out, val, eng="dve"):
        self.op(eng, lambda e: e.memset(out, val), [], [out])

    def recip(self, out, in_):
        self.op("dve", lambda e: e.reciprocal(out=out, in_=in_), [in_], [out])


class Rot:
    def __init__(self, bufs):
        self.bufs = bufs
        self.i = 0

    def get(self):
        b = self.bufs[self.i % len(self.bufs)]
        self.i += 1
        return b


def make_consts():
    s = np.arange(128)[:, None]
    t = np.arange(128)[None, :]
    same = (s // 64) == (t // 64)
    SU = ((s < t) & same).astype(np.float32)
    UI = ((s <= t) & same).astype(np.float32)
    SL = ((s > t) & same).astype(np.float32)
    causal = (s <= t).astype(np.float32)
    BD = same.astype(np.float32)
    ident = np.eye(128, dtype=np.float32)
    usum = np.zeros((128, 128), np.float32)
    usum[:64, 0] = -C0
    usum[64:, 1] = -C0
    ones = np.ones((128, 128), np.float32)
    ge = (s >= t).astype(np.float32)
    return np.concatenate([ident, SU, UI, SL, causal, BD, -C0 * UI, -C0 * SL, usum, ones, ge], axis=1)


CI_ID, CI_SU, CI_UI, CI_SL, CI_CAUS, CI_BD, CI_UCUM, CI_UREV, CI_USUM, CI_ONES, CI_GE = range(11)
NCST = 11 * 128

V_NORM = 0
V_MU_R = 8
V_MU_K = 12
V_MU_V = 16
V_MU_LW = 20
V_MU_LA = 21
V_MU_LG = 22
V_W0 = 23
V_A0 = 27
V_KK = 31
V_KA = 35
V_RK = 39
NV1 = 43


def build_L1():
    pb = PB()
    TW = 256
    NB = TW // 128
    xT = pb.din("xT", [D, TPC + 2])
    w_in = pb.din("w_in", [128, 8, 2816])
    vec_d = pb.din("vec", [128, NV1])
    rows_d = pb.din("rows", [128, 512 * 3 + 4 * TW])
    wsT_d = pb.din("wsT", [128, 512])
    w2_d = pb.din("w2", [64, 512])
    a2_d = pb.din("a2", [64, 512])
    g2_d = pb.din("g2", [128, 512])
    cst_d = pb.din("cst", [128, NCST])

    yaT_o = pb.dout("yaT", [512, TPC])
    olT_o = pb.dout("olT", [8, 64, TPC])
    rpT_o = pb.dout("rpT", [8, 64, TPC])
    bonT_o = pb.dout("bonT", [512, TPC])
    gT_o = pb.dout("gT", [512, TPC])
    Pn_o = pb.dout("Pn", [8, 64, 32 * 64])
    GT_o = pb.dout("GTn", [8, 64, 32 * 64])
    comp_o = pb.dout("comp", [8, 64, 128])

    cst = pb.sb("cst", [128, NCST])
    pb.dma(cst, cst_d, "sp")

    def CS(i, w=128):
        return cst[:, i * 128:i * 128 + w]

    ident = CS(CI_ID)
    vec = pb.sb("vecs", [128, NV1])
    pb.dma(vec, vec_d, "act")
    rows = pb.sb("rows", [128, 512 * 3 + 4 * TW])
    pb.dma(rows, rows_d, "pool")
    lng_row, lnb_row, w0_row = rows[:, 0:512], rows[:, 512:1024], rows[:, 1024:1536]
    bsbt = rows[:, 1536:1536 + 4 * TW]
    w2s = pb.sb("w2s", [64, 512])
    pb.dma(w2s, w2_d, "sp")
    a2s = pb.sb("a2s", [64, 512])
    pb.dma(a2s, a2_d, "act")
    g2s = pb.sb("g2s", [128, 512])
    pb.dma(g2s, g2_d, "pool")
    wsT = pb.sb("wsT", [128, 512])
    pb.dma(wsT, wsT_d, "sp")
    wsmb = pb.sb("wsmb", [128, 512], BF16)
    for g in range(4):
        pb.tt(wsmb[:, g * 128:(g + 1) * 128], wsT[:, g * 128:(g + 1) * 128], CS(CI_CAUS), ALU.mult)
    onesb = pb.sb("onesb", [128, 128], BF16)
    pb.cp(onesb, CS(CI_ONES), "dve")
    omka = pb.sb("omka", [128, 4])
    pb.ts(omka, vec[:, V_KA:V_KA + 4], -1.0, 1.0, ALU.mult, ALU.add)

    Wb = pb.sb("Wb", [128, 8, 2816], BF16)
    wst = Rot([pb.sb(f"wst{i}", [128, 704]) for i in range(2)])
    for kc in range(8):
        for hf in range(4):
            st = wst.get()
            pb.dma(st, w_in[:, kc, hf * 704:(hf + 1) * 704], "sp" if hf % 2 == 0 else "act")
            pb.cp(Wb[:, kc, hf * 704:(hf + 1) * 704], st, "pool" if hf % 2 == 0 else "dve")

    xs = pb.sb("xs", [128, 8, TW + 2])
    sqb = Rot([pb.sb(f"sqb{i}", [128, TW + 2], BF16) for i in range(2)])
    rstd = pb.sb("rstd", [128, TW + 2])
    hT = pb.sb("hT", [128, 8, TW + 2], BF16)
    uT = [pb.sb(f"uT{g}", [128, TW]) for g in range(4)]
    vnb = Rot([pb.sb(f"vnb{i}", [128, 512], BF16) for i in range(2)])
    tmpA = Rot([pb.sb(f"tmpA{i}", [128, 512]) for i in range(3)])
    st6 = pb.sb("st6", [128, 6])
    mv = pb.sb("mv", [128, 2])
    rsl = pb.sb("rsl", [128, 1])
    PSB = [pb.ps(f"psb{i}") for i in range(8)]

    pbuf = Rot([pb.sb(f"pbuf{i}", [128, TW + 2]) for i in range(4)])
    tl = pb.sb("tl", [64, TW])
    las = pb.sb("las", [64, TW])
    sgl = pb.sb("sgl", [128, TW])
    sgtok = pb.sb("sgtok", [128, NB, 512])
    rs = pb.sb("rs", [128, TW])
    ks = pb.sb("ks", [128, TW])
    vs = pb.sb("vs", [128, TW])
    sgT = pb.sb("sgT", [128, TW])
    ecum = pb.sb("ecum", [128, TW])
    encum = pb.sb("encum", [128, TW])
    eexcl = pb.sb("eexcl", [128, TW])
    erev = pb.sb("erev", [128, TW])
    kkn = pb.sb("kkn", [128, TW])
    al = pb.sb("al", [128, TW])
    kmod = pb.sb("kmod", [128, TW])
    bb = pb.sb("bb", [128, TW])
    bhT = pb.sb("bhT", [128, TW])
    khT = pb.sb("khT", [128, TW])
    tmpB = Rot([pb.sb(f"tmpB{i}", [128, 512]) for i in range(3)])
    AR = [pb.sb(f"AR{c}", [128, NB, 256]) for c in range(4)]
    BK = [pb.sb(f"BK{c}", [128, NB, 256]) for c in range(4)]
    gam = [pb.sb(f"gam{c}", [128, 2 * NB]) for c in range(4)]
    DG = [pb.sb(f"DG{c}", [128, 2 * NB, 128]) for c in range(4)]
    VT = pb.sb("VT", [128, NB, 512])
    BHT = pb.sb("BHT", [128, NB, 512])
    KHT = pb.sb("KHT", [128, NB, 512])
    AT = pb.sb("AT", [128, NB, 512])
    NR = 2
    MMt = [[pb.sb(f"MMt{r}_{i}", [128, 256]) for i in range(3)] for r in range(NR)]
    IMt = [Rot([pb.sb(f"IM{r}_{i}", [128, 128]) for i in range(2)]) for r in range(NR)]
    Yt = [[pb.sb(f"Y{r}_{i}", [128, 192]) for i in range(2)] for r in range(NR)]
    ABt = [pb.sb(f"AB{r}", [128, 192]) for r in range(NR)]
    AKt = [pb.sb(f"AK{r}", [128, 192]) for r in range(NR)]
    WKt = [pb.sb(f"WK{r}", [128, 192]) for r in range(NR)]
    RPs = Rot([pb.sb(f"RPs{i}", [64, 128]) for i in range(3)])
    OLs = Rot([pb.sb(f"OLs{i}", [64, 128]) for i in range(3)])
    Pst = Rot([pb.sb(f"Pst{i}", [64, 128]) for i in range(3)])
    GTs = Rot([pb.sb(f"GTs{i}", [64, 128]) for i in range(3)])
    Xst = [[pb.sb(f"X{h}", [64, 128])] * 2 for h in range(8)]
    xpar = [0] * 8
    for h in range(8):
        pb.memset(Xst[h][0][:, 0:64], 0.0, "pool")
        pb.cp(Xst[h][0][:, 64:128], ident[0:64, 0:64], "pool")
    outst = Rot([pb.sb(f"outst{i}", [128, TW]) for i in range(3)])

    psrot = Rot(PSB)
    evq = Rot(["act", "dve"])

    def proj_fm(col0, M, dst):
        pm = psrot.get()
        for kc in range(8):
            pb.mm(pm[0:M, 0:TW], Wb[:, kc, col0:col0 + M], hT[:, kc, 2:TW + 2], start=(kc == 0), stop=(kc == 7))
        pb.cp(dst[0:M, 2:TW + 2], pm[0:M, 0:TW], "act")
        ph = psrot.get()
        for kc in range(8):
            pb.mm(ph[0:M, 0:2], Wb[:, kc, col0:col0 + M], hT[:, kc, 0:2], start=(kc == 0), stop=(kc == 7))
        pb.cp(dst[0:M, 0:2], ph[0:M, 0:2], "dve")

    def shift(dst, p, M, mucol):
        d = tmpB.get()
        pb.tt(d[0:M, 0:TW], p[0:M, 1:TW + 1], p[0:M, 2:TW + 2], ALU.subtract)
        pb.stt(dst[0:M, 0:TW], d[0:M, 0:TW], vec[0:M, mucol:mucol + 1], p[0:M, 2:TW + 2], ALU.mult, ALU.add)

    rot_i = [0]

    for t in range(TPC // TW):
        c0 = t * TW
        pss = psrot.get()
        psh = psrot.get()
        for kc in range(8):
            pb.dma(xs[:, kc, :], xT[kc * 128:(kc + 1) * 128, c0:c0 + TW + 2], ("sp", "act", "pool")[kc % 3])
        for kc in range(8):
            sq = sqb.get()
            pb.act(sq, xs[:, kc, :], AF.Square)
            pb.mm(pss[:, 0:TW], onesb, sq[:, 2:TW + 2], start=(kc == 0), stop=(kc == 7))
        for kc in range(8):
            sq = sqb.get()
            pb.act(sq[:, 0:2], xs[:, kc, 0:2], AF.Square)
            pb.mm(psh[:, 0:2], onesb, sq[:, 0:2], start=(kc == 0), stop=(kc == 7))
        pb.act(rstd[:, 2:TW + 2], pss[:, 0:TW], AF.Sqrt, bias=1e-6, scale=1.0 / D)
        pb.act(rstd[:, 0:2], psh[:, 0:2], AF.Sqrt, bias=1e-6, scale=1.0 / D)
        pb.recip(rstd, rstd)
        for kc in range(8):
            pb.stt(hT[:, kc, :], xs[:, kc, :], vec[:, V_NORM + kc:V_NORM + kc + 1], rstd, ALU.mult, ALU.mult)

        for g in range(4):
            pm = psrot.get()
            for kc in range(8):
                pb.mm(pm[:, 0:TW], Wb[:, kc, g * 128:(g + 1) * 128], hT[:, kc, 2:TW + 2], start=(kc == 0), stop=(kc == 7))
            pb.cp(uT[g], pm[:, 0:TW], "act")
        psm = [psrot.get() for _ in range(4)]
        for b in range(NB):
            pv = psrot.get()
            for kc in range(8):
                pb.mm(pv, hT[:, kc, 2 + 128 * b:2 + 128 * (b + 1)], Wb[:, kc, 512:1024],
                      start=(kc == 0), stop=(kc == 7))
            pb.op("dve", lambda e: e.bn_stats(out=st6, in_=pv), [pv], [st6])
            pb.op("dve", lambda e: e.bn_aggr(out=mv, in_=st6), [st6], [mv])
            pb.act(rsl, mv[:, 1:2], AF.Sqrt, bias=1e-5, scale=1.0)
            pb.recip(rsl, rsl)
            t1 = tmpA.get()
            pb.ts(t1, pv, mv[:, 0:1], rsl, ALU.subtract, ALU.mult)
            t2 = tmpA.get()
            pb.tt(t2, t1, lng_row, ALU.mult, "pool")
            vn = vnb.get()
            pb.tt(vn, t2, lnb_row, ALU.add, "pool")
            for g in range(4):
                pb.mm(psm[g][:, b * 128:(b + 1) * 128], vn[:, g * 128:(g + 1) * 128],
                      wsmb[:, g * 128:(g + 1) * 128])
        for g in range(4):
            t1 = tmpA.get()
            pb.tt(t1[:, 0:TW], psm[g][:, 0:TW], bsbt[:, g * TW:(g + 1) * TW], ALU.add)
            o = outst.get()
            pb.tt(o, t1[:, 0:TW], uT[g], ALU.mult, "pool")
            pb.dma(yaT_o[g * 128:(g + 1) * 128, c0:c0 + TW], o, "pool")

        p = pbuf.get()
        proj_fm(2560, 64, p)
        d = tmpB.get()
        shift(d, p, 64, V_MU_LW)
        pb.act(tl, d[0:64, 0:TW], AF.Tanh)
        p = pbuf.get()
        proj_fm(2624, 64, p)
        shift(las, p, 64, V_MU_LA)
        p = pbuf.get()
        proj_fm(2688, 128, p)
        d = tmpB.get()
        shift(d, p, 128, V_MU_LG)
        pb.act(sgl, d[:, 0:TW], AF.Sigmoid)
        for b in range(NB):
            pm = psrot.get()
            pb.mm(pm, tl[:, b * 128:(b + 1) * 128], w2s)
            d = tmpB.get()
            pb.tt(d, pm, w0_row, ALU.add)
            pb.act(sgtok[:, b, :], d, AF.Sigmoid)

        for cc in range(4):
            AR4, BK4 = AR[cc], BK[cc]
            p = pbuf.get()
            proj_fm(1024 + cc * 128, 128, p)
            shift(rs, p, 128, V_MU_R + cc)
            p = pbuf.get()
            proj_fm(1536 + cc * 128, 128, p)
            shift(ks, p, 128, V_MU_K + cc)
            p = pbuf.get()
            proj_fm(2048 + cc * 128, 128, p)
            shift(vs, p, 128, V_MU_V + cc)
            ccs = slice(cc * 128, (cc + 1) * 128)
            pm = psrot.get()
            pb.mm(pm[:, 0:TW], w2s[:, ccs], tl)
            pb.act(sgT, pm[:, 0:TW], AF.Sigmoid, bias=vec[:, V_W0 + cc:V_W0 + cc + 1])
            psc = psrot.get()
            psr = psrot.get()
            pst = psrot.get()
            for b in range(NB):
                bs = slice(b * 128, (b + 1) * 128)
                pb.mm(psc[:, bs], sgtok[:, b, ccs], CS(CI_UCUM))
                pb.mm(psr[:, bs], sgtok[:, b, ccs], CS(CI_UREV))
                pb.mm(pst[:, 2 * b:2 * b + 2], sgtok[:, b, ccs], CS(CI_USUM, 2))
            pb.act(ecum, psc[:, 0:TW], AF.Exp)
            pb.act(encum, psc[:, 0:TW], AF.Exp, scale=-1.0)
            d = tmpB.get()
            pb.stt(d[:, 0:TW], sgT, C0, psc[:, 0:TW], ALU.mult, ALU.add)
            pb.act(eexcl, d[:, 0:TW], AF.Exp)
            pb.act(erev, psr[:, 0:TW], AF.Exp)
            pb.act(gam[cc], pst[:, 0:2 * NB], AF.Exp)
            for j in range(2 * NB):
                pb.ts(DG[cc][:, j, :], ident, gam[cc][:, j:j + 1], None, ALU.mult, eng="pool")
            kks = tmpB.get()[:, 0:TW]
            pb.ts(kks, ks, vec[:, V_KK + cc:V_KK + cc + 1], None, ALU.mult)
            d = tmpB.get()[:, 0:TW]
            pb.tt(d, kks, kks, ALU.mult, "pool")
            pm = psrot.get()
            pb.mm(pm[:, 0:TW], CS(CI_BD), d)
            d2 = tmpB.get()[:, 0:TW]
            pb.ts(d2, pm[:, 0:TW], 1e-24, None, ALU.max)
            pb.act(d2, d2, AF.Sqrt)
            pb.recip(d2, d2)
            pb.tt(kkn, kks, d2, ALU.mult)
            pm = psrot.get()
            pb.mm(pm[:, 0:TW], a2s[:, ccs], las)
            pb.act(al, pm[:, 0:TW], AF.Sigmoid, bias=vec[:, V_A0 + cc:V_A0 + cc + 1])
            d = tmpB.get()[:, 0:TW]
            pb.ts(d, al, vec[:, V_KA + cc:V_KA + cc + 1], omka[:, cc:cc + 1], ALU.mult, ALU.add)
            pb.tt(kmod, d, ks, ALU.mult)
            pb.tt(bb, kkn, al, ALU.mult, "pool")

            def v4(x):
                return x.rearrange("p (b t) -> p b t", t=128)

            pb.stt(AR4[:, :, 0:128], v4(kkn), -1.0, v4(eexcl), ALU.mult, ALU.mult)
            pb.tt(AR4[:, :, 128:256], v4(rs), v4(ecum), ALU.mult)
            pb.tt(BK4[:, :, 0:128], v4(bb), v4(encum), ALU.mult, "pool")
            pb.tt(BK4[:, :, 128:256], v4(kmod), v4(encum), ALU.mult)
            pb.tt(bhT, bb, erev, ALU.mult, "pool")
            pb.tt(khT, kmod, erev, ALU.mult, "pool")
            d = tmpB.get()[:, 0:TW]
            pb.stt(d, rs, vec[:, V_RK + cc:V_RK + cc + 1], kmod, ALU.mult, ALU.mult)
            pm = psrot.get()
            pb.mm(pm[:, 0:TW], CS(CI_BD), d)
            o = outst.get()
            pb.tt(o, pm[:, 0:TW], vs, ALU.mult)
            pb.dma(bonT_o[ccs, c0:c0 + TW], o, "pool")
            pm = psrot.get()
            pb.mm(pm[:, 0:TW], g2s[:, ccs], sgl)
            o = outst.get()
            pb.cp(o, pm[:, 0:TW], "act")
            pb.dma(gT_o[ccs, c0:c0 + TW], o, "pool")
            for src, dst in ((vs, VT), (bhT, BHT), (khT, KHT)):
                pm = psrot.get()
                for b in range(NB):
                    pb.tr(pm[:, b * 128:(b + 1) * 128], src[:, b * 128:(b + 1) * 128], ident)
                pb.cp(dst[:, :, ccs], pm[:, 0:TW].rearrange("p (b t) -> p b t", t=128), evq.get())
            pm = psrot.get()
            for b in range(NB):
                pb.tr(pm[:, b * 128:(b + 1) * 128], AR4[:, b, 0:128], ident)
            pb.cp(AT[:, :, ccs], pm[:, 0:TW].rearrange("p (b t) -> p b t", t=128), evq.get())

        for b in range(NB):
            for hd in range(8):
                cc = hd // 2
                hp = slice(64 * (hd % 2), 64 * (hd % 2) + 64)
                hc = slice(hd * 64, hd * 64 + 64)
                r = rot_i[0] % NR
                rot_i[0] += 1
                AR4, BK4 = AR[cc], BK[cc]
                MM, Y, AB, AK, WK = MMt[r], Yt[r], ABt[r], AKt[r], WKt[r]
                p1 = psrot.get()
                pb.mm(p1[:, 0:256], BK4[hp, b, 0:128], AR4[hp, b, :])
                pb.tt(MM[0][:, 0:128], p1[:, 0:128], CS(CI_SU), ALU.mult)
                pb.tt(AB[:, 0:128], p1[:, 128:256], CS(CI_UI), ALU.mult)
                p2 = psrot.get()
                pb.mm(p2[:, 0:256], AR4[hp, b, 0:128], BK4[hp, b, :])
                pb.tt(MM[0][:, 128:256], p2[:, 0:128], CS(CI_SL), ALU.mult)
                pb.tt(Y[0][:, 0:128], p2[:, 128:256], CS(CI_SL), ALU.mult)
                p3 = psrot.get()
                pb.mm(p3[:, 0:128], BK4[hp, b, 128:256], AR4[hp, b, 128:256])
                pb.tt(AK[:, 0:128], p3[:, 0:128], CS(CI_UI), ALU.mult)
                pb.cp(Y[0][:, 128:192], AT[:, b, hc], "pool")
                pb.cp(AB[:, 128:192], BHT[:, b, hc], "pool")
                pb.cp(AK[:, 128:192], KHT[:, b, hc], "pool")
                yc = 0
                for i in range(6):
                    im = IMt[r].get()
                    pb.tt(im, MM[i % 3][:, 0:128], ident, ALU.add, "pool")
                    py = psrot.get()
                    pb.mm(py[:, 0:192], im, Y[yc])
                    pb.cp(Y[1 - yc], py[:, 0:192], "act")
                    yc = 1 - yc
                    if i < 5:
                        pq = psrot.get()
                        pb.mm(pq[:, 0:128], MM[i % 3][:, 128:256], MM[i % 3][:, 0:128])
                        pb.mm(pq[:, 128:256], MM[i % 3][:, 0:128], MM[i % 3][:, 128:256])
                        pb.cp(MM[(i + 1) % 3], pq[:, 0:256], "dve")
                Yf = Y[yc]
                pf = psrot.get()
                pb.mm(pf[:, 0:192], Yf[:, 0:128], AB)
                pb.tt(WK, pf[:, 0:192], AK, ALU.add)
                pr = psrot.get()
                pb.mm(pr[0:64, 0:128], Yf[:, 128:192], AB[:, 0:128], start=True, stop=False)
                pb.mm(pr[0:64, 0:128], ident[:, hp], AR4[:, b, 128:256], start=False, stop=True)
                rp = RPs.get()
                pb.cp(rp, pr[0:64, 0:128], "act")
                tcol = c0 + b * 128
                pb.dma(rpT_o[hd, :, tcol:tcol + 128], rp, "sp")
                pp = psrot.get()
                for c in range(2):
                    cs_ = slice(c * 64, (c + 1) * 64)
                    pb.mm(pp[0:64, cs_], Yf[cs_, 128:192], AB[cs_, 128:192], start=True, stop=False)
                    pb.mm(pp[0:64, cs_], ident[:, hp], DG[cc][:, 2 * b + c, hp], start=False, stop=True)
                ps_ = Pst.get()
                pb.cp(ps_, pp[0:64, 0:128], "dve")
                n0 = (t * NB + b) * 2
                pb.dma(Pn_o[hd, :, n0 * 64:(n0 + 2) * 64], ps_, "act")
                po = psrot.get()
                pb.mm(po[0:64, 0:128], VT[:, b, hc], WK[:, 0:128])
                ol = OLs.get()
                pb.cp(ol, po[0:64, 0:128], "act")
                pb.dma(olT_o[hd, :, tcol:tcol + 128], ol, "sp")
                pg = psrot.get()
                for c in range(2):
                    cs_ = slice(c * 64, (c + 1) * 64)
                    pb.mm(pg[0:64, cs_], WK[cs_, 128:192], VT[cs_, b, hc])
                gt = GTs.get()
                pb.cp(gt, pg[0:64, 0:128], "dve")
                pb.dma(GT_o[hd, :, n0 * 64:(n0 + 2) * 64], gt, "act")
                for c in range(2):
                    cs_ = slice(c * 64, (c + 1) * 64)
                    xo = Xst[hd][xpar[hd]]
                    xn = Xst[hd][1 - xpar[hd]]
                    px = psrot.get()
                    pb.mm(px[0:64, 0:128], ps_[:, cs_], xo)
                    pb.tt(xn[:, 0:64], px[0:64, 0:64], gt[:, cs_], ALU.add)
                    pb.cp(xn[:, 64:128], px[0:64, 64:128], "act")
                    xpar[hd] = 1 - xpar[hd]

    for hd in range(8):
        pb.dma(comp_o[hd], Xst[hd][xpar[hd]], "sp")
    pb.finish()
    return pb


def load_cast_w(pb, dst, src_d, ncols, nk=8, piece=512, name="wl"):
    st = Rot([pb.sb(f"{name}_st{i}", [128, piece]) for i in range(3)])
    engs = Rot(["sp", "act", "pool"])
    cps = Rot(["pool", "dve", "act"])
    for kc in range(nk):
        for c0 in range(0, ncols, piece):
            w = min(piece, ncols - c0)
            b = st.get()
            pb.dma(b[:, 0:w], src_d[:, kc, c0:c0 + w], engs.get())
            pb.cp(dst[:, kc, c0:c0 + w], b[:, 0:w], cps.get())


def outproj_norm(pb, yT, Wo, xres_d, normvec, x_out_d, h_out_d, psrot, pssrot, onesb, name):
    xn = pb.sb(name + "_xn", [128, 8, 512])
    xr = Rot([pb.sb(f"{name}_xr{i}", [128, 512]) for i in range(2)])
    sqb = Rot([pb.sb(f"{name}_sq{i}", [128, 512], BF16) for i in range(2)])
    rstd = pb.sb(name + "_rstd", [128, 512])
    hb = Rot([pb.sb(f"{name}_hb{i}", [128, 512], BF16) for i in range(2)])
    dq = Rot(["sp", "act", "pool"])
    for tt in range(TPC // 512):
        ts_ = slice(tt * 512, (tt + 1) * 512)
        pss = pssrot.get()
        for oc in range(8):
            pm = psrot.get()
            for kc in range(8):
                pb.mm(pm, Wo[:, kc, oc * 128:(oc + 1) * 128], yT[kc][:, ts_], start=(kc == 0), stop=(kc == 7))
            x_ = xr.get()
            pb.dma(x_, xres_d[oc * 128:(oc + 1) * 128, ts_], dq.get())
            pb.tt(xn[:, oc, :], pm, x_, ALU.add)
            pb.dma(x_out_d[oc * 128:(oc + 1) * 128, ts_], xn[:, oc, :], dq.get())
            sq = sqb.get()
            pb.act(sq, xn[:, oc, :], AF.Square)
            pb.mm(pss, onesb, sq, start=(oc == 0), stop=(oc == 7))
        pb.act(rstd, pss, AF.Sqrt, bias=1e-6, scale=1.0 / D)
        pb.recip(rstd, rstd)
        for oc in range(8):
            h_ = hb.get()
            pb.stt(h_, xn[:, oc, :], normvec[:, oc:oc + 1], rstd, ALU.mult, ALU.mult)
            pb.dma(h_out_d[oc * 128:(oc + 1) * 128, ts_], h_, dq.get())


V2_GNG = 0
V2_GNB = 4
V2_FFN = 8
NV2 = 16


def build_L2():
    pb = PB()
    comp_d = pb.din("compall", [64, 8, 8, 128])
    sel_d = pb.din("sel", [64, 8])
    Pn_d = pb.din("Pn", [8, 64, 2048])
    GT_d = pb.din("GTn", [8, 64, 2048])
    ol_d = pb.din("olT", [8, 64, TPC])
    rp_d = pb.din("rpT", [8, 64, TPC])
    bon_d = pb.din("bonT", [512, TPC])
    g_d = pb.din("gT", [512, TPC])
    ya_d = pb.din("yaT", [512, TPC])
    x_d = pb.din("xT0", [D, TPC])
    wo_d = pb.din("w_out", [128, 8, 1024])
    vec_d = pb.din("vec", [128, NV2])
    cst_d = pb.din("cst", [128, NCST])
    x_o = pb.dout("x0pT", [D, TPC])
    h_o = pb.dout("hfT", [D, TPC], BF16)
    dbg_o = pb.dout("dbg_yb", [512, TPC], BF16) if DEBUG else None
    dbg2_o = pb.dout("dbg_o", [512, TPC]) if DEBUG else None

    cst = pb.sb("cst", [128, NCST])
    pb.dma(cst, cst_d, "sp")

    def CS(i, w=128):
        return cst[:, i * 128:i * 128 + w]

    ident = CS(CI_ID)
    vec = pb.sb("vecs", [128, NV2])
    pb.dma(vec, vec_d, "act")
    onesb = pb.sb("onesb", [128, 128], BF16)
    pb.cp(onesb, CS(CI_ONES), "dve")
    PSB = [pb.ps(f"psb{i}") for i in range(8)]
    psrot = Rot(PSB[0:6])
    psorot = Rot(PSB[6:8])
    Wo = pb.sb("Wo", [128, 8, 1024], BF16)
    load_cast_w(pb, Wo, wo_d, 1024)
    yT = [pb.sb(f"yT{i}", [128, TPC], BF16) for i in range(8)]
    stg = Rot([pb.sb(f"stg{i}", [128, 512]) for i in range(2)])
    dq = Rot(["sp", "act", "pool"])
    for kc in range(4):
        for tt in range(4):
            b = stg.get()
            pb.dma(b, ya_d[kc * 128:(kc + 1) * 128, tt * 512:(tt + 1) * 512], dq.get())
            pb.cp(yT[kc][:, tt * 512:(tt + 1) * 512], b, ("pool", "act")[tt % 2])

    comp = pb.sb("comp", [64, 8, 8, 128])
    pb.dma(comp, comp_d, "sp")
    sel = pb.sb("sel", [64, 8])
    pb.dma(sel, sel_d, "act")
    MT = pb.sb("MT", [64, 7, 512])
    Z = pb.sb("Z", [64, 8, 512])
    pb.memset(Z[:, 0, :], 0.0)
    for c in range(7):
        pm = psrot.get()
        for h in range(8):
            pb.tr(pm[0:64, h * 64:(h + 1) * 64], comp[:, c, h, 64:128], ident[0:64, 0:64])
        pb.cp(MT[:, c, :], pm[0:64, :], "act")
    for c in range(7):
        pm = psrot.get()
        for h in range(8):
            pb.mm(pm[0:64, h * 64:(h + 1) * 64], MT[:, c, h * 64:(h + 1) * 64], Z[:, c, h * 64:(h + 1) * 64])
        pb.tt(Z[:, c + 1, :].rearrange("p (h v) -> p h v", v=64), pm[0:64, :].rearrange("p (h v) -> p h v", v=64),
              comp[:, c, :, 0:64], ALU.add)
    Ssel = pb.sb("Ssel", [64, 512])
    pb.ts(Ssel, Z[:, 1, :], sel[:, 1:2], None, ALU.mult)
    for c in range(2, 8):
        pb.stt(Ssel, Z[:, c, :], sel[:, c:c + 1], Ssel, ALU.mult, ALU.add)

    NS = 4
    Sst = [pb.sb(f"Sst{i}", [128, 64]) for i in range(NS)]
    big = [Rot([pb.sb(f"{nm}{i}", [128, TPC]) for i in range(1)]) for nm in ("Pp", "Gp", "Rp", "Op")]
    oT = pb.sb("oT", [128, TPC])
    tB = Rot([pb.sb(f"tB{i}", [128, 512]) for i in range(5)])
    for pr in range(4):
        Pp, Gp, Rp, Op = [r.get() for r in big]
        for dst, src in ((Pp, Pn_d), (Gp, GT_d), (Rp, rp_d), (Op, ol_d)):
            pb.dma(dst, src[2 * pr:2 * pr + 2].rearrange("h k n -> (h k) n"), dq.get())
        si = 0
        pb.cp(Sst[0][0:64, :], Ssel[:, (2 * pr) * 64:(2 * pr + 1) * 64], "act")
        pb.cp(Sst[0][64:128, :], Ssel[:, (2 * pr + 1) * 64:(2 * pr + 2) * 64], "act")
        pso = None
        for n in range(32):
            S = Sst[si % NS]
            ns = slice(n * 64, (n + 1) * 64)
            if n % 8 == 0:
                pso = psorot.get()
            for hh in range(2):
                hp = slice(hh * 64, hh * 64 + 64)
                pb.mm(pso[hp, (n % 8) * 64:(n % 8 + 1) * 64], S[hp, :], Rp[hp, ns])
            if n < 31:
                psn = psrot.get()
                for hh in range(2):
                    hp = slice(hh * 64, hh * 64 + 64)
                    pb.mm(psn[hp, 0:64], Pp[hp, ns], S[hp, :])
                Sn = Sst[(si + 1) % NS]
                pb.tt(Sn, psn[:, 0:64], Gp[:, ns], ALU.add)
                si += 1
            if n % 8 == 7:
                t0 = (n // 8) * 512
                pb.tt(oT[:, t0:t0 + 512], pso, Op[:, t0:t0 + 512], ALU.add)
        for tt in range(4):
            ts_ = slice(tt * 512, (tt + 1) * 512)
            bon = tB.get()
            pb.dma(bon, bon_d[pr * 128:(pr + 1) * 128, ts_], dq.get())
            gg = tB.get()
            pb.dma(gg, g_d[pr * 128:(pr + 1) * 128, ts_], dq.get())
            pm = psrot.get()
            pb.mm(pm, CS(CI_BD), oT[:, ts_])
            d = tB.get()
            pb.stt(d, pm, -1.0 / 64, oT[:, ts_], ALU.mult, ALU.add)
            sq = tB.get()
            pb.tt(sq, d, d, ALU.mult, "pool")
            pv = psrot.get()
            pb.mm(pv, CS(CI_BD), sq)
            rs_ = tB.get()
            pb.act(rs_, pv, AF.Sqrt, bias=64e-5, scale=1.0 / 64)
            pb.recip(rs_, rs_)
            pb.tt(d, d, rs_, ALU.mult)
            pb.ts(d, d, vec[:, V2_GNG + pr:V2_GNG + pr + 1], vec[:, V2_GNB + pr:V2_GNB + pr + 1], ALU.mult, ALU.add)
            pb.tt(d, d, bon, ALU.add, "pool")
            pb.tt(yT[4 + pr][:, ts_], d, gg, ALU.mult, "pool")
        if DEBUG:
            pb.dma(dbg_o[pr * 128:(pr + 1) * 128, :], yT[4 + pr], "sp")
            pb.dma(dbg2_o[pr * 128:(pr + 1) * 128, :], oT, "sp")

    outproj_norm(pb, yT, Wo, x_d, vec[:, V2_FFN:V2_FFN + 8], x_o, h_o, psrot, psorot, onesb, "op")
    pb.finish()
    return pb


NJ = 22
HALF = 1024


def build_FFN(final):
    pb = PB()
    hf_d = pb.din("hfh", [D, TPC + 2], BF16)
    x_d = pb.din("xres", [D, TPC])
    wup_d = pb.din("w_up", [128, 8, 5632])
    wdn_d = pb.din("w_down", [128, NJ, 1024])
    cv_d = pb.din("convv", [128, 44 * 4])
    nv_d = pb.din("normv", [128, 8])
    cst_d = pb.din("cst", [128, NCST])
    if final:
        out_o = pb.dout("outT", [D, TPC])
    else:
        x_o = pb.dout("xnewT", [D, TPC])
        h_o = pb.dout("hnT", [D, TPC], BF16)

    cst = pb.sb("cst", [128, NCST])
    pb.dma(cst, cst_d, "sp")
    onesb = pb.sb("onesb", [128, 128], BF16)
    pb.cp(onesb, cst[:, CI_ONES * 128:(CI_ONES + 1) * 128], "dve")
    cv = pb.sb("cv", [128, 44 * 4])
    pb.dma(cv, cv_d, "act")
    nv = pb.sb("nv", [128, 8])
    pb.dma(nv, nv_d, "act")
    PSB = [pb.ps(f"psb{i}") for i in range(8)]
    psrot = Rot(PSB[0:6])
    pssrot = Rot(PSB[6:8])
    dq = Rot(["sp", "act", "pool"])

    Wd = pb.sb("Wd", [128, NJ, 1024], BF16)
    wdst = Rot([pb.sb(f"wdst{i}", [128, 1024]) for i in range(2)])
    hf = pb.sb("hf", [128, 8, HALF + 2], BF16)
    aT = [pb.sb(f"aT{j}", [128, HALF], BF16) for j in range(NJ)]
    wst = Rot([pb.sb(f"wst{i}", [128, 8, 256]) for i in range(2)])
    wbf = Rot([pb.sb(f"wbf{i}", [128, 8, 256], BF16) for i in range(2)])
    zb = [Rot([pb.sb(f"zb{g}{i}", [128, 514]) for i in range(2)]) for g in range(2)]
    acc = [Rot([pb.sb(f"acc{g}{i}", [128, 512]) for i in range(2)]) for g in range(2)]
    sgb = Rot([pb.sb(f"sgb{i}", [128, 512]) for i in range(2)])
    xn = pb.sb("xn", [128, 8, 512])
    xr = Rot([pb.sb(f"xr{i}", [128, 512]) for i in range(2)])
    sqb = Rot([pb.sb(f"sq{i}", [128, 512], BF16) for i in range(2)])
    rstd = pb.sb("rstd", [128, 512])
    hb = Rot([pb.sb(f"hb{i}", [128, 512], BF16 if not final else F32) for i in range(2)])

    wd_loaded = [False]

    def load_wd():
        cps = Rot(["pool", "dve"])
        for j in range(NJ):
            b = wdst.get()
            pb.dma(b, wdn_d[:, j, :], dq.get())
            pb.cp(Wd[:, j, :], b, cps.get())

    for half in range(TPC // HALF):
        h0 = half * HALF
        for kc in range(8):
            pb.dma(hf[:, kc, :], hf_d[kc * 128:(kc + 1) * 128, h0:h0 + HALF + 2], dq.get())
        for j in range(NJ):
            st = wst.get()
            pb.dma(st[:, :, 0:128], wup_d[:, :, j * 128:(j + 1) * 128], "sp")
            pb.dma(st[:, :, 128:256], wup_d[:, :, 2816 + j * 128:2816 + (j + 1) * 128], "act")
            wb = wbf.get()
            pb.cp(wb[:, :, 0:128], st[:, :, 0:128], "pool")
            pb.cp(wb[:, :, 128:256], st[:, :, 128:256], "pool")
            prev = [None, None]
            for tt in range(HALF // 512):
                t0 = tt * 512
                accs = []
                for g in range(2):
                    q = j if g == 0 else NJ + j
                    cc = cv[:, q * 4:(q + 1) * 4]
                    z = zb[g].get()
                    pm = psrot.get()
                    for kc in range(8):
                        pb.mm(pm, wb[:, kc, g * 128:(g + 1) * 128], hf[:, kc, 2 + t0:2 + t0 + 512],
                              start=(kc == 0), stop=(kc == 7))
                    if tt == 0:
                        ph = psrot.get()
                        for kc in range(8):
                            pb.mm(ph[:, 0:2], wb[:, kc, g * 128:(g + 1) * 128], hf[:, kc, 0:2],
                                  start=(kc == 0), stop=(kc == 7))
                        pb.cp(z[:, 0:2], ph[:, 0:2], "dve")
                    else:
                        pb.cp(z[:, 0:2], prev[g][:, 512:514], "pool")
                    pb.cp(z[:, 2:514], pm, "act")
                    a_ = acc[g].get()
                    pb.act(a_, pm, AF.Identity, bias=cc[:, 3:4], scale=cc[:, 2:3])
                    pb.stt(a_, z[:, 1:513], cc[:, 1:2], a_, ALU.mult, ALU.add)
                    pb.stt(a_, z[:, 0:512], cc[:, 0:1], a_, ALU.mult, ALU.add)
                    prev[g] = z
                    accs.append(a_)
                sg = sgb.get()
                pb.act(sg, accs[0], AF.Silu)
                pb.tt(aT[j][:, t0:t0 + 512], sg, accs[1], ALU.mult, "pool")
        if not wd_loaded[0]:
            load_wd()
            wd_loaded[0] = True
        for tt in range(HALF // 512):
            t0 = tt * 512
            gs = slice(h0 + t0, h0 + t0 + 512)
            pss = pssrot.get()
            for oc in range(8):
                pm = psrot.get()
                for j in range(NJ):
                    pb.mm(pm, Wd[:, j, oc * 128:(oc + 1) * 128], aT[j][:, t0:t0 + 512],
                          start=(j == 0), stop=(j == NJ - 1))
                x_ = xr.get()
                pb.dma(x_, x_d[oc * 128:(oc + 1) * 128, gs], dq.get())
                pb.tt(xn[:, oc, :], pm, x_, ALU.add)
                if not final:
                    pb.dma(x_o[oc * 128:(oc + 1) * 128, gs], xn[:, oc, :], dq.get())
                sq = sqb.get()
                pb.act(sq, xn[:, oc, :], AF.Square)
                pb.mm(pss, onesb, sq, start=(oc == 0), stop=(oc == 7))
            pb.act(rstd, pss, AF.Sqrt, bias=1e-6, scale=1.0 / D)
            pb.recip(rstd, rstd)
            for oc in range(8):
                h_ = hb.get()
                pb.stt(h_, xn[:, oc, :], nv[:, oc:oc + 1], rstd, ALU.mult, ALU.mult)
                pb.dma((out_o if final else h_o)[oc * 128:(oc + 1) * 128, gs], h_, dq.get())
    pb.finish()
    return pb


def build_QKV():
    pb = PB()
    h_d = pb.din("h1T", [D, TPC], BF16)
    w_d = pb.din("w_qkv", [128, 8, 3072])
    q_o = pb.dout("qT", [D, TPC], BF16)
    k_o = pb.dout("kT", [D, TPC], BF16)
    v_o = pb.dout("vtok", [TPC, D], BF16)
    PSB = [pb.ps(f"psb{i}") for i in range(8)]
    psrot = Rot(PSB)
    dq = Rot(["sp", "act", "pool"])
    h1 = pb.sb("h1", [128, 8, TPC], BF16)
    for kc in range(8):
        pb.dma(h1[:, kc, :], h_d[kc * 128:(kc + 1) * 128, :], dq.get())
    W = pb.sb("W", [128, 8, 3072], BF16)
    load_cast_w(pb, W, w_d, 3072)
    ob = Rot([pb.sb(f"ob{i}", [128, 512], BF16) for i in range(4)])
    evq = Rot(["act", "dve"])
    for c in range(16):
        dst = q_o if c < 8 else k_o
        r0 = (c % 8) * 128
        for tt in range(TPC // 512):
            pm = psrot.get()
            for kc in range(8):
                pb.mm(pm, W[:, kc, c * 128:(c + 1) * 128], h1[:, kc, tt * 512:(tt + 1) * 512],
                      start=(kc == 0), stop=(kc == 7))
            o = ob.get()
            pb.cp(o, pm, evq.get())
            pb.dma(dst[r0:r0 + 128, tt * 512:(tt + 1) * 512], o, dq.get())
    for tb in range(TPC // 128):
        for hf_ in range(2):
            pm = psrot.get()
            for kc in range(8):
                pb.mm(pm, h1[:, kc, tb * 128:(tb + 1) * 128], W[:, kc, 2048 + hf_ * 512:2048 + (hf_ + 1) * 512],
                      start=(kc == 0), stop=(kc == 7))
            o = ob.get()
            pb.cp(o, pm, evq.get())
            pb.dma(v_o[tb * 128:(tb + 1) * 128, hf_ * 512:(hf_ + 1) * 512], o, dq.get())
    pb.finish()
    return pb


def build_ATT():
    pb = PB()
    q_d = pb.din("qT", [D, TPC], BF16)
    k_d = pb.din("kTh", [D, 2 * TPC], BF16)
    v_d = pb.din("vh", [2 * TPC, D], BF16)
    x_d = pb.din("xres", [D, TPC])
    wo_d = pb.din("w_out", [128, 8, 1024])
    nv_d = pb.din("normv", [128, 8])
    fl_d = pb.din("flag", [128, 1])
    cst_d = pb.din("cst", [128, NCST])
    x_o = pb.dout("xnewT", [D, TPC])
    h_o = pb.dout("hnT", [D, TPC], BF16)
    dbg_o = pb.dout("dbg_y", [D, TPC], BF16) if DEBUG else None

    cst = pb.sb("cst", [128, NCST])
    pb.dma(cst, cst_d, "sp")

    def CS(i, w=128):
        return cst[:, i * 128:i * 128 + w]

    onesb = pb.sb("onesb", [128, 128], BF16)
    pb.cp(onesb, CS(CI_ONES), "dve")
    nv = pb.sb("nv", [128, 8])
    pb.dma(nv, nv_d, "act")
    fl = pb.sb("fl", [128, 1])
    pb.dma(fl, fl_d, "act")
    flb = pb.sb("flb", [128, 128], BF16)
    pb.ts(flb, CS(CI_ONES), fl[:, 0:1], None, ALU.mult)
    maskab = pb.sb("maskab", [128, 256], BF16)
    pb.cp(maskab[:, 0:128], CS(CI_GE), "dve")
    pb.cp(maskab[:, 128:256], CS(CI_CAUS), "dve")
    PSB = [pb.ps(f"psb{i}") for i in range(8)]
    psrot = Rot(PSB[0:6])
    pssrot = Rot(PSB[6:8])
    dq = Rot(["sp", "act", "pool"])
    Wo = pb.sb("Wo", [128, 8, 1024], BF16)
    load_cast_w(pb, Wo, wo_d, 1024)
    yT = [pb.sb(f"yT{i}", [128, TPC], BF16) for i in range(8)]
    qs = Rot([pb.sb(f"qs{i}", [128, TPC], BF16) for i in range(2)])
    ks_ = Rot([pb.sb(f"ks{i}", [128, 2 * TPC], BF16) for i in range(2)])
    vst = [Rot([pb.sb(f"vst{v}_{i}", [128, 32, 128], BF16) for i in range(2)]) for v in range(3)]
    accs = Rot([pb.sb(f"acc{i}", [128, TPC]) for i in range(2)])
    Pt = Rot([pb.sb(f"Pt{i}", [128, 256], BF16) for i in range(4)])
    Pm = Rot([pb.sb(f"Pm{i}", [128, 256], BF16) for i in range(4)])
    dsh = Rot([pb.sb(f"dsh{i}", [128, TPC]) for i in range(1)])
    mq = Rot(["pool", "dve"])
    DIL = (1, 4, 16)
    for cc in range(8):
        q = qs.get()
        pb.dma(q, q_d[cc * 128:(cc + 1) * 128, :], dq.get())
        k = ks_.get()
        pb.dma(k, k_d[cc * 128:(cc + 1) * 128, :], dq.get())
        V = []
        for vi, d in enumerate(DIL):
            vt = vst[vi].get()
            nmb = 32 // d
            src = v_d[:, cc * 128:(cc + 1) * 128].rearrange("(mb p dd) c -> dd p mb c", p=128, dd=d)
            for r in range(d):
                pb.dma(vt[:, r * nmb:(r + 1) * nmb, :], src[r], dq.get())
            V.append(vt)
        for hh in range(2):
            hp = slice(hh * 64, hh * 64 + 64)
            npart = hp
            dpart = slice(64, 128) if hh == 0 else slice(0, 64)
            acc = accs.get()
            for vi, d in enumerate(DIL):
                nmb = 32 // d
                nqb = 16 // d
                for r in range(d):
                    for qb in range(nqb):
                        qsl = slice(r + d * 128 * qb, r + d * 128 * qb + d * 127 + 1, d)
                        kb0 = 2048 + r + d * 128 * qb
                        ka0 = kb0 - d * 128
                        vtb = r * nmb + (16 // d) + qb
                        vta = vtb - 1
                        ps1 = psrot.get()
                        pb.mm(ps1[:, 0:128], k[hp, ka0:ka0 + d * 127 + 1:d], q[hp, qsl])
                        pb.mm(ps1[:, 128:256], k[hp, kb0:kb0 + d * 127 + 1:d], q[hp, qsl])
                        p_ = Pt.get()
                        pb.act(p_, ps1[:, 0:256], AF.Exp, scale=0.125)
                        pm_ = Pm.get()
                        pb.tt(pm_, p_, maskab, ALU.mult, mq.get())
                        ps2 = psrot.get()
                        pb.mm(ps2[npart, 0:128], V[vi][:, vta, hp], pm_[:, 0:128], start=True, stop=False)
                        pb.mm(ps2[npart, 0:128], V[vi][:, vtb, hp], pm_[:, 128:256], start=False, stop=True)
                        pb.mm(ps2[dpart, 0:128], (flb if qb == 0 else onesb)[:, 0:64], pm_[:, 0:128],
                              start=True, stop=False)
                        pb.mm(ps2[dpart, 0:128], onesb[:, 0:64], pm_[:, 128:256], start=False, stop=True)
                        if d == 1:
                            pb.cp(acc[:, qsl], ps2[:, 0:128], "act")
                        else:
                            pb.tt(acc[:, qsl], ps2[:, 0:128], acc[:, qsl], ALU.add)
            ds_ = dsh.get()
            pb.cp(ds_[npart, :], acc[dpart, :], "act")
            pb.recip(ds_[npart, :], ds_[npart, :])
            pb.tt(yT[cc][npart, :], acc[npart, :], ds_[npart, :], ALU.mult)
        if DEBUG:
            pb.dma(dbg_o[cc * 128:(cc + 1) * 128, :], yT[cc], "sp")
    outproj_norm(pb, yT, Wo, x_d, nv, x_o, h_o, psrot, pssrot, onesb, "op")
    pb.finish()
    return pb

def _cols(v, n):
    return np.ascontiguousarray(np.asarray(v, np.float32).reshape(n, 128).T)


def _halo_T(XT, c, halo):
    out = np.zeros((XT.shape[0], halo + TPC), XT.dtype)
    lo = c * TPC - halo
    if lo < 0:
        out[:, -lo:] = XT[:, 0:c * TPC + TPC]
    else:
        out[:, :] = XT[:, lo:c * TPC + TPC]
    return out


def prep_L1(inp):
    xTf = np.ascontiguousarray(inp["x"][0].T)
    w_in = np.ascontiguousarray(inp["ev_w_in"][0].reshape(8, 128, 2816).transpose(1, 0, 2))
    mu = inp["ev_mu"][0]
    vec = np.zeros((128, NV1), np.float32)
    vec[:, V_NORM:V_NORM + 8] = _cols(inp["ev_norm"][0], 8)
    vec[:, V_MU_R:V_MU_R + 4] = _cols(mu[0:512], 4)
    vec[:, V_MU_K:V_MU_K + 4] = _cols(mu[512:1024], 4)
    vec[:, V_MU_V:V_MU_V + 4] = _cols(mu[1024:1536], 4)
    vec[0:64, V_MU_LW] = mu[1536:1600]
    vec[0:64, V_MU_LA] = mu[1600:1664]
    vec[:, V_MU_LG] = mu[1664:1792]
    vec[:, V_W0:V_W0 + 4] = _cols(inp["ev_w0"][0], 4)
    vec[:, V_A0:V_A0 + 4] = _cols(inp["ev_a0"][0], 4)
    vec[:, V_KK:V_KK + 4] = _cols(inp["ev_k_k"][0], 4)
    vec[:, V_KA:V_KA + 4] = _cols(inp["ev_k_a"][0], 4)
    vec[:, V_RK:V_RK + 4] = _cols(inp["ev_r_k"][0].reshape(512), 4)
    TW = 256
    rows = np.zeros((128, 512 * 3 + 4 * TW), np.float32)
    rows[:, 0:512] = inp["ev_ln_g"][0][None, :]
    rows[:, 512:1024] = inp["ev_ln_b"][0][None, :]
    rows[:, 1024:1536] = inp["ev_w0"][0][None, :]
    for g in range(4):
        rows[:, 1536 + g * TW:1536 + (g + 1) * TW] = np.tile(inp["ev_b_s"][0][g], TW // 128)[None, :]
    wsT = np.ascontiguousarray(np.concatenate([inp["ev_w_s"][0][g].T for g in range(4)], axis=1))
    shared = {"w_in": w_in, "vec": vec, "rows": rows, "wsT": wsT,
              "w2": np.ascontiguousarray(inp["ev_w2"][0]), "a2": np.ascontiguousarray(inp["ev_a2"][0]),
              "g2": np.ascontiguousarray(inp["ev_g2"][0]), "cst": make_consts()}
    maps = []
    for c in range(NCORE):
        m = dict(shared)
        m["xT"] = _halo_T(xTf, c, 2)
        maps.append(m)
    return maps


def prep_L2(inp, r1, xTf):
    compall = np.ascontiguousarray(np.stack([r1[c]["comp"] for c in range(NCORE)], 0).transpose(2, 0, 1, 3))
    vec = np.zeros((128, NV2), np.float32)
    vec[:, V2_GNG:V2_GNG + 4] = _cols(inp["ev_gn_g"][0], 4)
    vec[:, V2_GNB:V2_GNB + 4] = _cols(inp["ev_gn_b"][0], 4)
    vec[:, V2_FFN:V2_FFN + 8] = _cols(inp["ff_norm"][0], 8)
    w_out = np.ascontiguousarray(inp["ev_w_out"][0].reshape(8, 128, 1024).transpose(1, 0, 2))
    cst = make_consts()
    maps = []
    for c in range(NCORE):
        sel = np.zeros((64, 8), np.float32)
        sel[:, c] = 1.0
        m = {"compall": compall, "sel": sel, "w_out": w_out, "vec": vec, "cst": cst,
             "xT0": np.ascontiguousarray(xTf[:, c * TPC:(c + 1) * TPC])}
        for k in ("Pn", "GTn", "olT", "rpT", "bonT", "gT", "yaT"):
            m[k] = r1[c][k]
        maps.append(m)
    return maps


def _bf16_halo(parts, halo):
    outs = []
    for c in range(NCORE):
        o = np.zeros((parts[c].shape[0], halo + TPC), parts[c].dtype)
        o[:, halo:] = parts[c]
        if c > 0:
            o[:, :halo] = parts[c - 1][:, TPC - halo:]
        outs.append(o)
    return outs


def prep_FFN(inp, layer, hparts, xparts, normvec):
    w_up = np.ascontiguousarray(inp["ff_w_up"][layer].reshape(8, 128, 5632).transpose(1, 0, 2))
    w_dn = np.ascontiguousarray(inp["ff_w_down"][layer].reshape(NJ, 128, 1024).transpose(1, 0, 2))
    cw = inp["ff_conv_w"][layer]
    cb = inp["ff_conv_b"][layer]
    cvv = np.stack([cw[0], cw[1], cw[2], cb], axis=-1)
    cvv = np.ascontiguousarray(cvv.reshape(44, 128, 4).transpose(1, 0, 2).reshape(128, 44 * 4)).astype(np.float32)
    hh = _bf16_halo(hparts, 2)
    cst = make_consts()
    nv = _cols(normvec, 8)
    return [{"hfh": hh[c], "xres": xparts[c], "w_up": w_up, "w_down": w_dn, "convv": cvv,
             "normv": nv, "cst": cst} for c in range(NCORE)]


def prep_QKV(inp, hparts):
    w = np.ascontiguousarray(inp["od_w_qkv"][0].reshape(8, 128, 3072).transpose(1, 0, 2))
    return [{"h1T": hparts[c], "w_qkv": w} for c in range(NCORE)]


def prep_ATT(inp, rq, xparts):
    w_out = np.ascontiguousarray(inp["od_w_out"][0].reshape(8, 128, 1024).transpose(1, 0, 2))
    nv = _cols(inp["ff_norm"][1], 8)
    cst = make_consts()
    kh = _bf16_halo([rq[c]["kT"] for c in range(NCORE)], TPC)
    maps = []
    for c in range(NCORE):
        v = rq[c]["vtok"]
        vh = np.zeros((2 * TPC, D), v.dtype)
        vh[TPC:] = v
        if c > 0:
            vh[:TPC] = rq[c - 1]["vtok"]
        fl = np.full((128, 1), 0.0 if c == 0 else 1.0, np.float32)
        maps.append({"qT": rq[c]["qT"], "kTh": kh[c], "vh": vh, "xres": xparts[c], "w_out": w_out,
                     "normv": nv, "flag": fl, "cst": cst})
    return maps


def _run(pb, maps):
    res = run_bass_kernel_spmd(pb.nc, maps, core_ids=list(range(NCORE)))
    return res.results


def kernel(**inputs):
    inp = {k: np.asarray(v) for k, v in inputs.items()}
    xTf = np.ascontiguousarray(inp["x"][0].T)
    r1 = _run(build_L1(), prep_L1(inp))
    r2 = _run(build_L2(), prep_L2(inp, r1, xTf))
    r3 = _run(build_FFN(False),
              prep_FFN(inp, 0, [r2[c]["hfT"] for c in range(NCORE)], [r2[c]["x0pT"] for c in range(NCORE)],
                       inp["od_norm"][0]))
    rq = _run(build_QKV(), prep_QKV(inp, [r3[c]["hnT"] for c in range(NCORE)]))
    r4 = _run(build_ATT(), prep_ATT(inp, rq, [r3[c]["xnewT"] for c in range(NCORE)]))
    r5 = _run(build_FFN(True),
              prep_FFN(inp, 1, [r4[c]["hnT"] for c in range(NCORE)], [r4[c]["xnewT"] for c in range(NCORE)],
                       inp["final_norm"]))
    outT = np.concatenate([np.asarray(r5[c]["outT"], np.float32) for c in range(NCORE)], axis=1)
    return np.ascontiguousarray(outT.T)[None, :, :]
```
